# Optimizing a Trainium2 kernel written in Bass

```python
import math
import jax, jax.numpy as jnp
from jax import lax
import numpy as np

D_MODEL = 2048
BATCH = 2
SEQ = 4096
DEPTH = 2

N_A_LAYERS = DEPTH // 2
N_B_LAYERS = DEPTH - N_A_LAYERS
GLA_HEADS = 4
GLA_KEY_DIM = D_MODEL // 2
GLA_VAL_DIM = D_MODEL
GLA_DK = GLA_KEY_DIM // GLA_HEADS
GLA_DV = GLA_VAL_DIM // GLA_HEADS
GATE_RANK = 16
GATE_NORMALIZER = 16.0
GLA_CHUNK = 64
GLA_IN_DIM = 2 * GLA_KEY_DIM + 2 * GLA_VAL_DIM + GATE_RANK
ATT_HEADS = 16
HEAD_DIM = D_MODEL // ATT_HEADS
WINDOWS = (128, 512, 2048)
DILATIONS = (1, 4, 16)
N_BRANCH = 3
ATT_BLOCK = 128
D_FF = 5632
CONV_WIDTH = 3
EPS = 1e-6

kernel_name = "yoco_gla_dilated_swa_convglu"


def rmsnorm(x, g):
    x32 = x.astype(jnp.float32)
    y = x32 * lax.rsqrt(jnp.mean(x32 * x32, axis=-1, keepdims=True) + EPS)
    return (y * g.astype(jnp.float32)).astype(x.dtype)


def alibi_slopes(n):
    def pow2_slopes(m):
        start = 2.0 ** (-8.0 / m)
        return [start ** (i + 1) for i in range(m)]
    if math.log2(n).is_integer():
        s = pow2_slopes(n)
    else:
        c = 2 ** math.floor(math.log2(n))
        s = pow2_slopes(c) + pow2_slopes(2 * c)[0::2][: n - c]
    return jnp.asarray(np.array(s, dtype=np.float32))


def gla_mixer(h, w_in, w_a2, b_a2, head_norm, w_out):
    bsz, s_len, _ = h.shape
    n_chunks = s_len // GLA_CHUNK
    f32 = jnp.float32
    proj = h @ w_in
    q, k, v, r, a = jnp.split(
        proj, [GLA_KEY_DIM, 2 * GLA_KEY_DIM, 2 * GLA_KEY_DIM + GLA_VAL_DIM,
               2 * GLA_KEY_DIM + 2 * GLA_VAL_DIM], axis=-1)
    log_alpha = jax.nn.log_sigmoid((a @ w_a2 + b_a2).astype(f32)) / GATE_NORMALIZER

    def chunks(t, hd):
        return t.astype(f32).reshape(bsz, n_chunks, GLA_CHUNK, GLA_HEADS, hd).transpose(1, 0, 3, 2, 4)

    qc = chunks(q, GLA_DK) * (GLA_DK ** -0.5)
    kc = chunks(k, GLA_DK)
    vc = chunks(v, GLA_DV)
    cum = jnp.cumsum(chunks(log_alpha, GLA_DK), axis=3)
    last = cum[:, :, :, -1:, :]
    q_dec = qc * jnp.exp(cum)
    k_inv = kc * jnp.exp(-cum)
    k_to_end = kc * jnp.exp(last - cum)

    causal = jnp.tril(jnp.ones((GLA_CHUNK, GLA_CHUNK), dtype=bool))
    scores = jnp.where(causal, jnp.einsum('nbhtk,nbhsk->nbhts', q_dec, k_inv), 0.0)
    o_intra = jnp.einsum('nbhts,nbhsv->nbhtv', scores, vc)

    def step(state, xs):
        q_n, k_n, v_n, dec_n = xs
        o_n = jnp.einsum('bhtk,bhkv->bhtv', q_n, state)
        state = state * dec_n[..., None] + jnp.einsum('bhsk,bhsv->bhkv', k_n, v_n)
        return state, o_n

    state0 = jnp.zeros((bsz, GLA_HEADS, GLA_DK, GLA_DV), f32)
    _, o_inter = lax.scan(step, state0, (q_dec, k_to_end, vc, jnp.exp(last[:, :, :, 0, :])))
    o = (o_intra + o_inter).transpose(1, 0, 3, 2, 4).reshape(bsz, s_len, GLA_HEADS, GLA_DV)
    o = rmsnorm(o, head_norm)
    gate = jax.nn.silu(r.astype(f32)).reshape(bsz, s_len, GLA_HEADS, GLA_DV)
    o = (o * gate).reshape(bsz, s_len, GLA_VAL_DIM).astype(h.dtype)
    return o @ w_out


def to_dilated(t, d):
    bsz, s_len, nh, e = t.shape
    return t.reshape(bsz, s_len // d, d, nh, e).transpose(0, 2, 1, 3, 4)


def n_blocks(sub_len):
    return -(-sub_len // ATT_BLOCK)


def to_query_blocks(t, d):
    td = to_dilated(t, d)
    bsz, _, sub_len, nh, e = td.shape
    nb = n_blocks(sub_len)
    td = jnp.pad(td, ((0, 0), (0, 0), (0, nb * ATT_BLOCK - sub_len), (0, 0), (0, 0)))
    return td.reshape(bsz, d, nb, ATT_BLOCK, nh, e)


def to_key_blocks(t, d):
    td = to_dilated(t, d)
    bsz, _, sub_len, nh, e = td.shape
    nb = n_blocks(sub_len)
    td = jnp.pad(td, ((0, 0), (0, 0), (ATT_BLOCK, nb * ATT_BLOCK - sub_len), (0, 0), (0, 0)))
    return td.reshape(bsz, d, nb + 1, ATT_BLOCK, nh, e)


def from_blocks(t, d, s_len):
    bsz, _, nb, _, nh, e = t.shape
    t = t.reshape(bsz, d, nb * ATT_BLOCK, nh, e)[:, :, : s_len // d]
    return t.transpose(0, 2, 1, 3, 4).reshape(bsz, s_len, nh, e)


def shared_kv(h, kv_norm, w_kv):
    bsz, s_len, _ = h.shape
    kv = rmsnorm(h, kv_norm) @ w_kv
    k, v = jnp.split(kv, 2, axis=-1)
    k = k.reshape(bsz, s_len, ATT_HEADS, HEAD_DIM)
    v = v.reshape(bsz, s_len, ATT_HEADS, HEAD_DIM)
    return [(to_key_blocks(k, d), to_key_blocks(v, d)) for d in DILATIONS]


def dilated_branch(qb, kb, vb, d, keys_back, slopes):
    nb = qb.shape[2]
    s_prev = jnp.einsum('brnqhe,brnkhe->brnhqk', qb, kb[:, :, :-1])
    s_cur = jnp.einsum('brnqhe,brnkhe->brnhqk', qb, kb[:, :, 1:])
    s = jnp.concatenate([s_prev, s_cur], axis=-1).astype(jnp.float32) * (HEAD_DIM ** -0.5)
    qa = jnp.arange(ATT_BLOCK)
    kc = jnp.arange(2 * ATT_BLOCK)
    j = qa[:, None] - kc[None, :] + ATT_BLOCK
    key_sub = jnp.arange(nb)[:, None] * ATT_BLOCK - ATT_BLOCK + kc[None, :]
    valid = ((j >= 0) & (j <= keys_back))[None] & (key_sub >= 0)[:, None, :]
    alibi = -slopes[:, None, None] * (j * d).astype(jnp.float32)[None]
    s = jnp.where(valid[None, None, :, None], s + alibi[None, None, None], -jnp.inf)
    m = jnp.max(s, axis=-1, keepdims=True)
    p = jnp.exp(s - m)
    l = jnp.sum(p, axis=-1, keepdims=True)
    o = (jnp.einsum('brnhqk,brnkhe->brnqhe', p[..., :ATT_BLOCK], vb[:, :, :-1])
         + jnp.einsum('brnhqk,brnkhe->brnqhe', p[..., ATT_BLOCK:], vb[:, :, 1:]))
    o = o / l.transpose(0, 1, 2, 4, 3, 5)
    lse = (m + jnp.log(l)).transpose(0, 1, 2, 4, 3, 5)
    return o, lse


def dilated_mixer(h, kv_blocks, w_q, w_out):
    bsz, s_len, _ = h.shape
    q = (h @ w_q).reshape(bsz, s_len, N_BRANCH, ATT_HEADS, HEAD_DIM)
    slopes = alibi_slopes(ATT_HEADS)
    outs, lses = [], []
    for g in range(N_BRANCH):
        d = DILATIONS[g]
        kb, vb = kv_blocks[g]
        o, lse = dilated_branch(to_query_blocks(q[:, :, g], d), kb, vb, d, WINDOWS[g] // d, slopes)
        outs.append(from_blocks(o, d, s_len))
        lses.append(from_blocks(lse, d, s_len))
    w = jax.nn.softmax(jnp.stack(lses, axis=0), axis=0)
    o = jnp.sum(w * jnp.stack(outs, axis=0), axis=0)
    return o.reshape(bsz, s_len, ATT_HEADS * HEAD_DIM).astype(h.dtype) @ w_out


def conv_glu(h, w_up, conv_w, conv_b, w_down):
    u, g = jnp.split(h @ w_up, 2, axis=-1)
    gp = jnp.pad(g, ((0, 0), (CONV_WIDTH - 1, 0), (0, 0)))
    g = conv_w[0] * gp[:, :-2] + conv_w[1] * gp[:, 1:-1] + conv_w[2] * gp[:, 2:] + conv_b
    return (jax.nn.gelu(g, approximate=False) * u) @ w_down


def setup_inputs(seed: int = 0) -> dict:
    key = jax.random.key(seed)
    ks = jax.random.split(key, 17)
    f32 = jnp.float32

    def nrm(k, shape, fan_in):
        return jax.random.normal(k, shape, f32) * (fan_in ** -0.5)

    def gain(k, shape):
        return 1.0 + 0.02 * jax.random.normal(k, shape, f32)

    return {
        "x": jax.random.normal(ks[0], (BATCH, SEQ, D_MODEL), f32),
        "attn_norm": gain(ks[1], (DEPTH, D_MODEL)),
        "gla_w_in": nrm(ks[2], (N_A_LAYERS, D_MODEL, GLA_IN_DIM), D_MODEL),
        "gla_w_a2": nrm(ks[3], (N_A_LAYERS, GATE_RANK, GLA_KEY_DIM), GATE_RANK),
        "gla_b_a2": 0.1 * jax.random.normal(ks[4], (N_A_LAYERS, GLA_KEY_DIM), f32),
        "gla_head_norm": gain(ks[5], (N_A_LAYERS, GLA_DV)),
        "gla_w_out": nrm(ks[6], (N_A_LAYERS, GLA_VAL_DIM, D_MODEL), GLA_VAL_DIM),
        "kv_norm": gain(ks[7], (D_MODEL,)),
        "w_kv": nrm(ks[8], (D_MODEL, 2 * ATT_HEADS * HEAD_DIM), D_MODEL),
        "dsa_w_q": nrm(ks[9], (N_B_LAYERS, D_MODEL, N_BRANCH * ATT_HEADS * HEAD_DIM), D_MODEL),
        "dsa_w_out": nrm(ks[10], (N_B_LAYERS, ATT_HEADS * HEAD_DIM, D_MODEL), ATT_HEADS * HEAD_DIM),
        "ffn_norm": gain(ks[11], (DEPTH, D_MODEL)),
        "ffn_w_up": nrm(ks[12], (DEPTH, D_MODEL, 2 * D_FF), D_MODEL),
        "ffn_conv_w": nrm(ks[13], (DEPTH, CONV_WIDTH, D_FF), CONV_WIDTH),
        "ffn_conv_b": 0.02 * jax.random.normal(ks[14], (DEPTH, D_FF), f32),
        "ffn_w_down": nrm(ks[15], (DEPTH, D_FF, D_MODEL), D_FF),
        "final_norm": gain(ks[16], (D_MODEL,)),
    }


def reference(x, attn_norm, gla_w_in, gla_w_a2, gla_b_a2, gla_head_norm, gla_w_out,
              kv_norm, w_kv, dsa_w_q, dsa_w_out, ffn_norm, ffn_w_up, ffn_conv_w,
              ffn_conv_b, ffn_w_down, final_norm):
    h = x
    kv_blocks = None
    for i in range(DEPTH):
        if i < N_A_LAYERS:
            h = h + gla_mixer(rmsnorm(h, attn_norm[i]), gla_w_in[i], gla_w_a2[i], gla_b_a2[i],
                              gla_head_norm[i], gla_w_out[i])
        else:
            if i == N_A_LAYERS:
                kv_blocks = shared_kv(h, kv_norm, w_kv)
            j = i - N_A_LAYERS
            h = h + dilated_mixer(rmsnorm(h, attn_norm[i]), kv_blocks, dsa_w_q[j], dsa_w_out[j])
        h = h + conv_glu(rmsnorm(h, ffn_norm[i]), ffn_w_up[i], ffn_conv_w[i], ffn_conv_b[i],
                         ffn_w_down[i])
    return rmsnorm(h, final_norm)
```

```python
import numpy as np
import ml_dtypes
import concourse.bass as bass
import concourse.mybir as mybir
from concourse.bass_utils import run_bass_kernel_spmd
from contextlib import ExitStack

F32 = mybir.dt.float32
BF16 = mybir.dt.bfloat16
I32 = mybir.dt.int32
ALU = mybir.AluOpType
AF = mybir.ActivationFunctionType
POOL_ENG = mybir.EngineType.Pool
SP_ENG = mybir.EngineType.SP

ENG = ("pe", "act", "dve", "pool", "sp")

D = 2048
T = 1024
NT = 8
KC = 16
DFF = 5632
NFP = 11
EPS = 1e-6


class Tile:
    __slots__ = ("name", "lw", "rd")

    def __init__(self, name):
        self.name = name
        self.lw = None
        self.rd = []


class Prog:
    NDMA = {"sp": 24, "act": 4, "pool": 16}

    def __init__(self, nc, es):
        self.nc = nc
        self.es = es
        self.q = {e: [] for e in ENG}
        self.sems = {}
        self.cnt = {}
        self.seen = {e: {} for e in ENG}
        self.pending = {e: ([], []) for e in ENG}
        for e in ("pe", "act", "dve", "pool"):
            self.sems[e] = es.enter_context(nc.semaphore("c_" + e))
            self.cnt[e] = 0
        self.dma_i = {}
        for qn, n in self.NDMA.items():
            for i in range(n):
                k = "d_%s%d" % (qn, i)
                self.sems[k] = es.enter_context(nc.semaphore(k))
                self.cnt[k] = 0
            self.dma_i[qn] = 0
        self.n_ins = 0
        self.all_events = {}

    def sbuf(self, name, shape, dt):
        return self.es.enter_context(self.nc.sbuf_tensor("s_" + name, list(shape), dt))

    def _wait(self, eng, ev):
        if ev is None:
            return
        k, v = ev
        if self.seen[eng].get(k, 0) >= v:
            return
        self.seen[eng][k] = v
        self.q[eng].append(("w", k, v))

    def _deps(self, eng, reads, writes):
        for t in reads:
            self._wait(eng, t.lw)
        for t in writes:
            self._wait(eng, t.lw)
            for ev in t.rd:
                self._wait(eng, ev)

    def _commit(self, ev, reads, writes):
        self.all_events[ev[0]] = ev[1]
        for t in writes:
            t.lw = ev
            t.rd = []
        for t in reads:
            t.rd.append(ev)
            if len(t.rd) > 16:
                best = {}
                for k, v in t.rd:
                    if best.get(k, 0) < v:
                        best[k] = v
                t.rd = list(best.items())

    def op(self, eng, fn, reads=(), writes=(), signal=True):
        self._deps(eng, reads, writes)
        self.n_ins += 1
        pr, pw = self.pending[eng]
        if not signal:
            self.q[eng].append(("i", fn, None, 0))
            pr.extend(reads)
            pw.extend(writes)
            return None
        self.cnt[eng] += 1
        ev = (eng, self.cnt[eng])
        self.q[eng].append(("i", fn, eng, 1))
        self._commit(ev, list(reads) + pr, list(writes) + pw)
        self.pending[eng] = ([], [])
        return ev

    def dma(self, qn, fn, reads=(), writes=(), inc=16):
        n = self.NDMA[qn]
        i = self.dma_i[qn]
        self.dma_i[qn] = (i + 1) % n
        k = "d_%s%d" % (qn, i)
        if self.cnt[k] > 0:
            self._wait(qn, (k, self.cnt[k]))
        self._deps(qn, reads, writes)
        self.cnt[k] += inc
        ev = (k, self.cnt[k])
        self.q[qn].append(("i", fn, k, inc))
        self._commit(ev, list(reads), list(writes))
        self.n_ins += 1
        return ev

    def wait_all(self, eng, tiles):
        for t in tiles:
            self._wait(eng, t.lw)
            for ev in t.rd:
                self._wait(eng, ev)

    def barrier(self):
        for e in ENG:
            for k, v in list(self.all_events.items()):
                self._wait(e, (k, v))

    def finalize(self):
        nc = self.nc
        sems = self.sems
        q = self.q

        def replay(e, items):
            for it in items:
                if it[0] == "w":
                    e.wait_ge(sems[it[1]], it[2])
                else:
                    ins = it[1](e)
                    if it[2] is not None:
                        ins.then_inc(sems[it[2]], it[3])

        with nc.Block() as block:
            @block.tensor
            def _(e):
                replay(e, q["pe"])

            @block.scalar
            def _(e):
                replay(e, q["act"])

            @block.vector
            def _(e):
                replay(e, q["dve"])

            @block.gpsimd
            def _(e):
                replay(e, q["pool"])

            @block.sync
            def _(e):
                replay(e, q["sp"])


C_GAIN = 0
C_BA2 = 80
C_CONVW = 88
C_CONVB = C_CONVW + 264
C_SEL = C_CONVB + 88
C_HALO = C_SEL + 3
C_KB = C_HALO + 1
NCST = C_KB + 3


class K:
    def __init__(self, stop="full"):
        self.stop = stop
        self.nc = nc = bass.Bass("TRN2", target_bir_lowering=False)
        self.es = ExitStack()
        self.P = Prog(nc, self.es)
        self.din = {}
        self._bank = 0
        P = self.P
        self.h = P.sbuf("h", [128, NT, D], F32)
        self.t_h = [Tile("h%d" % i) for i in range(NT)]
        self.xnT = P.sbuf("xnT", [128, KC, T + 2], BF16)
        self.t_xnT = Tile("xnT")
        self.t_halo = Tile("halo")
        self.cst = P.sbuf("cst", [128, NCST], F32)
        self.t_cst = Tile("cst")
        self.idf = P.sbuf("idf", [128, 128], F32)
        self.idb = P.sbuf("idb", [128, 128], BF16)
        self.t_id = Tile("id")
        self.stat = P.sbuf("stat", [128, 8], F32)
        self.t_stat = Tile("stat")
        self.idx = P.sbuf("idx", [1, 8], I32)
        self.t_idx = Tile("idx")
        self.slots = []
        self.t_slots = []
        self._slot = 0
        self._nphase = 0
        self.ps = self.es.enter_context(nc.psum_tensor("ps", [128, 8, 512], F32))
        self.t_ps = [Tile("ps%d" % i) for i in range(8)]
        self.vals = {}

    def alloc_slots(self, es, n):
        self._nphase += 1
        self.slots = [es.enter_context(self.nc.sbuf_tensor("slot%d_%d" % (self._nphase, i), [128, 8192], BF16)) for i in range(n)]
        self.t_slots = [Tile("slot%d" % i) for i in range(n)]
        self._slot = 0

    def inp(self, name, shape, dt=F32):
        if name not in self.din:
            self.din[name] = self.nc.dram_tensor(name, list(shape), dt, kind="ExternalInput").ap()
        return self.din[name]

    def bank(self, n=1):
        b = self._bank
        if n == 2 and b % 2:
            b += 1
        if b + n > 8:
            b = 0
        self._bank = (b + n) % 8
        return b

    def psflat(self, b, n, c0=0, c1=None):
        v = self.ps[:, b:b + n, :].rearrange("p a b -> p (a b)")
        if c1 is None:
            c1 = n * 512
        return v[:, c0:c1]

    def next_slot(self):
        s = self._slot
        self._slot = (s + 1) % len(self.slots)
        return s

    def load_w(self, src_ap, s, view):
        n = src_ap.shape[-1]
        if n > 512:
            for c0 in range(0, n, 512):
                self.P.dma("pool", lambda e, c0=c0: e.dma_start(out=view[:, :, c0:c0 + 512], in_=src_ap[:, :, c0:c0 + 512]),
                           writes=[self.t_slots[s]])
        else:
            self.P.dma("pool", lambda e: e.dma_start(out=view, in_=src_ap), writes=[self.t_slots[s]])

    def allgather(self, src, dst, t_src, t_dst):
        ev = self.P.dma("pool", lambda e: e.collective_compute("AllGather", ALU.bypass, replica_groups=[list(range(8))],
                                                                ins=[src[:, :]], outs=[dst[:, :]]),
                        reads=[t_src], writes=[t_dst], inc=1)
        self.P._wait("pool", ev)

    def slot3(self, s, a, b):
        return self.slots[s][:, 0:a * b].rearrange("p (a b) -> p a b", a=a)

    def load_inputs(self):
        P = self.P
        x = self.inp("x", [T, D])
        cst = self.inp("cst", [128, NCST])
        ident = self.inp("ident", [128, 128])
        idx = self.inp("idx", [1, 8], I32)
        P.dma("sp", lambda e: e.dma_start(out=self.cst[:], in_=cst[:, :]), writes=[self.t_cst])
        P.dma("sp", lambda e: e.dma_start(out=self.idf[:], in_=ident[:, :]), writes=[self.t_id])
        P.dma("sp", lambda e: e.dma_start(out=self.idx[:], in_=idx[:, :]), writes=[self.t_idx])
        for i in range(NT):
            P.dma("sp", lambda e, i=i: e.dma_start(out=self.h[:, i, :], in_=x[i * 128:(i + 1) * 128, :]),
                  writes=[self.t_h[i]])
        P.op("dve", lambda e: e.tensor_copy(out=self.idb[:], in_=self.idf[:]), reads=[self.t_id], writes=[self.t_id])


    def dyn_init(self):
        nc = self.nc
        self.r = {nm: self.es.enter_context(nc.sync.register("r_" + nm)) for nm in ("rm1", "rm2", "gb", "t0", "t1", "t2", "t3")}
        self._rt = 0

        def ld(e):
            for j, nm in enumerate(("rm1", "rm2", "gb")):
                e.reg_load(self.r[nm], self.idx[0:1, j:j + 1])
            return None
        self.P.op("sp", ld, reads=[self.t_idx], signal=False)

    def dyn_dma(self, out_ap, tensor, base, mult, add, pattern, reads, writes):
        rt = self.r["t%d" % self._rt]
        self._rt = (self._rt + 1) % 4

        def fn(e):
            e.reg_mul(rt, self.r[base], mult)
            if add:
                e.reg_add(rt, rt, add)
            return e.dma_start(out=out_ap, in_=bass.AP(tensor, rt, pattern))
        return self.P.dma("sp", fn, reads=reads, writes=writes)

    def norm(self, gcol):
        P = self.P
        es = ExitStack()
        self._nphase += 1
        hn2 = [es.enter_context(self.nc.sbuf_tensor("hn%d_%d" % (self._nphase, i), [128, D], F32)) for i in range(2)]
        t_hn2 = [Tile("hn0"), Tile("hn1")]
        st = es.enter_context(self.nc.sbuf_tensor("nst%d" % self._nphase, [128, NT, 4], F32))
        t_st = [Tile("nst%d" % i) for i in range(NT)]
        for i in range(NT):
            hn = hn2[i % 2]; t_hn = t_hn2[i % 2]
            P.op("act", lambda e, i=i, hn=hn: e.activation(out=hn[:], in_=self.h[:, i, :], func=AF.Square,
                                                           accum_out=st[:, i, 0:1]),
                 reads=[self.t_h[i]], writes=[t_hn, t_st[i]])
            P.op("dve", lambda e, i=i: e.tensor_scalar(out=st[:, i, 1:2], in0=st[:, i, 0:1], scalar1=1.0 / D,
                                                       scalar2=EPS, op0=ALU.mult, op1=ALU.add),
                 reads=[t_st[i]], writes=[t_st[i]])
            P.op("act", lambda e, i=i: e.activation(out=st[:, i, 2:3], in_=st[:, i, 1:2], func=AF.Sqrt),
                 reads=[t_st[i]], writes=[t_st[i]])
            P.op("dve", lambda e, i=i: e.reciprocal(out=st[:, i, 3:4], in_=st[:, i, 2:3]),
                 reads=[t_st[i]], writes=[t_st[i]])
            P.op("dve", lambda e, i=i, hn=hn: e.tensor_scalar(out=hn[:], in0=self.h[:, i, :], scalar1=st[:, i, 3:4],
                                                              scalar2=None, op0=ALU.mult),
                 reads=[self.t_h[i], t_st[i]], writes=[t_hn])
            for g4 in range(4):
                b = self.bank()
                for j in range(4):
                    kc = g4 * 4 + j
                    P.op("pe", lambda e, b=b, j=j, kc=kc, hn=hn: e.transpose(out=self.ps[:, b, j * 128:(j + 1) * 128],
                                                                           in_=hn[:, kc * 128:(kc + 1) * 128],
                                                                           identity=self.idf[:]),
                         reads=[t_hn, self.t_id], writes=[self.t_ps[b]], signal=(j == 3))
                P.op("dve", lambda e, b=b, g4=g4, i=i: e.tensor_tensor(
                    out=self.xnT[:, g4 * 4:(g4 + 1) * 4, 2 + i * 128:2 + (i + 1) * 128],
                    in0=self.ps[:, b, :].rearrange("p (a b) -> p a b", a=4),
                    in1=self.cst[:, gcol + g4 * 4:gcol + (g4 + 1) * 4].unsqueeze(2).to_broadcast([128, 4, 128]),
                    op=ALU.mult),
                    reads=[self.t_ps[b], self.t_cst], writes=[self.t_xnT])
        P.barrier()
        es.close()

    def gla(self):
        P = self.P
        nc = self.nc
        w_in = self.inp("gla_w_in", [D, 6160])
        w_a2 = self.inp("gla_w_a2", [16, 1024])
        hnb_d = self.inp("hnb", [128, 512])
        w_out = self.inp("gla_w_out", [D, D])
        cm_d = self.inp("cmask", [128, 128])
        winr = w_in.rearrange("(kc p) n -> p kc n", p=128)
        es = ExitStack()
        self.es.enter_context(es)

        def sb(name, shape, dt):
            return es.enter_context(nc.sbuf_tensor("g_" + name, list(shape), dt))
        self.alloc_slots(es, 2)
        wa = sb("wa", [128, KC, 16], BF16); t_wa = Tile("wa")
        wa2 = sb("wa2", [16, 1024], BF16); t_wa2 = Tile("wa2")
        aT = sb("aT", [16, T], BF16); t_aT = Tile("aT")
        negb = sb("negb", [128, 8], F32); t_negb = Tile("negb")
        hnb = sb("hnb", [128, 512], F32); t_hnb = Tile("hnb")
        cmask = sb("cmask", [128, 128], F32); t_cm = Tile("cmask")
        ones = sb("ones", [128, 128], F32); t_ones = Tile("ones")
        t1 = sb("t1", [128, 2, T], F32); t_t1 = [Tile("t1a"), Tile("t1b")]
        t2 = sb("t2", [128, T], F32); t_t2 = Tile("t2")
        dec = sb("dec", [128, 2, 8], F32); t_dec = Tile("dec")
        qdT = sb("qdT", [128, 2, T], BF16); t_qdT = Tile("qdT")
        kiT = sb("kiT", [128, 2, T], BF16); t_kiT = Tile("kiT")
        ktmp = sb("ktmp", [128, 8, 128], BF16); t_ktmp = Tile("ktmp")
        kte = sb("kte", [128, 8, 256], BF16); t_kte = Tile("kte")
        v_tm = sb("v_tm", [128, 8, 512], BF16); t_v = Tile("v_tm")
        gate = sb("gate", [128, 8, 512], BF16); t_gate = Tile("gate")
        ftmp = sb("ftmp", [128, 512], F32); t_ftmp = Tile("ftmp")
        gtmp = [ftmp, ftmp]; t_gtmp = [t_ftmp, t_ftmp]
        S = sb("S", [128, 2, 512], F32); t_S = Tile("S")
        Sbf = sb("Sbf", [128, 2, 512], BF16); t_Sbf = Tile("Sbf")
        Dl = sb("Dl", [128, 2, 8], F32); t_Dl = Tile("Dl")
        R = [sb("R%d" % i, [128, 2, 520], F32) for i in range(2)]; t_R = [[Tile("R0a"), Tile("R0b")], [Tile("R1a"), Tile("R1b")]]
        pT = [sb("pT%d" % i, [128, 128], BF16) for i in range(2)]; t_pT = [Tile("pT0"), Tile("pT1")]
        og = [sb("og%d" % i, [128, 512], BF16) for i in range(2)]; t_og = [Tile("og0"), Tile("og1")]
        ogT = [sb("ogT%d" % i, [128, 4, 128], BF16) for i in range(2)]; t_ogT = [Tile("ogT0"), Tile("ogT1")]
        st2 = sb("st2", [128, 8], F32); t_st2 = Tile("st2")
        gsrc = [nc.dram_tensor("gsrc%d" % i, [256, 520], F32).ap() for i in range(4)]
        gdst_h = [nc.dram_tensor("gdst%d" % i, [8 * 256, 520], F32, addr_space="Shared") for i in range(4)]
        gdst = [g.ap() for g in gdst_h]
        t_gsrc = [Tile("gsrc%d" % i) for i in range(4)]
        t_gdst = [Tile("gdst%d" % i) for i in range(4)]
        gstage = [nc.dram_tensor("gstage%d" % i, [768, 520], F32).ap() for i in range(4)]
        t_gstage = [Tile("gstage%d" % i) for i in range(4)]

        P.dma("pool", lambda e: e.dma_start(out=wa[:], in_=winr[:, :, 6144:6160]), writes=[t_wa])
        P.dma("pool", lambda e: e.dma_start(out=wa2[:], in_=w_a2[:, :]), writes=[t_wa2])
        P.dma("sp", lambda e: e.dma_start(out=hnb[:], in_=hnb_d[:, :]), writes=[t_hnb])
        P.dma("sp", lambda e: e.dma_start(out=cmask[:], in_=cm_d[:, :]), writes=[t_cm])
        P.op("dve", lambda e: e.memset(ones[:], 1.0), writes=[t_ones])
        P.op("dve", lambda e: e.tensor_scalar(out=negb[:], in0=self.cst[:, C_BA2:C_BA2 + 8], scalar1=-1.0, scalar2=None,
                                              op0=ALU.mult), reads=[self.t_cst], writes=[t_negb])
        b = self.bank(2)
        for half in range(2):
            for kc in range(KC):
                P.op("pe", lambda e, half=half, kc=kc, b=b: e.matmul(
                    self.ps[0:16, b + half, :], lhsT=wa[:, kc, :], rhs=self.xnT[:, kc, 2 + half * 512:2 + (half + 1) * 512],
                    start=(kc == 0), stop=(kc == KC - 1)),
                    reads=[t_wa, self.t_xnT], writes=[self.t_ps[b + half]], signal=(kc == KC - 1))
        P.op("act", lambda e, b=b: e.copy(out=aT[:], in_=self.ps[0:16, b:b + 2, :].rearrange("p a b -> p (a b)")),
             reads=[self.t_ps[b], self.t_ps[b + 1]], writes=[t_aT])

        import os
        DBG = int(os.environ.get("KDBG", "99"))
        if DBG <= 2:
            P.barrier(); es.close(); return
        for hd in range(4):
            sqk = 0
            v3 = self.slot3(sqk, KC, 512)
            self.load_w(winr[:, :, hd * 256:(hd + 1) * 256], sqk, v3[:, :, 0:256])
            self.load_w(winr[:, :, 1024 + hd * 256:1024 + (hd + 1) * 256], sqk, v3[:, :, 256:512])
            sv = 1
            self.load_w(winr[:, :, 2048 + hd * 512:2048 + (hd + 1) * 512], sv, self.slot3(sv, KC, 512))
            for jl in range(2):
                jj = 2 * hd + jl
                b = self.bank(2)
                for half in range(2):
                    P.op("pe", lambda e, half=half, b=b, jj=jj: e.matmul(
                        self.ps[:, b + half, :], lhsT=wa2[0:16, jj * 128:(jj + 1) * 128],
                        rhs=aT[0:16, half * 512:(half + 1) * 512], start=True, stop=True),
                        reads=[t_wa2, t_aT], writes=[self.t_ps[b + half]])
                P.op("act", lambda e, b=b, jj=jj, jl=jl: e.activation(out=t1[:, jl, :], in_=self.psflat(b, 2), func=AF.Exp,
                                                                      bias=negb[:, jj:jj + 1], scale=-1.0),
                     reads=[self.t_ps[b], self.t_ps[b + 1], t_negb], writes=[t_t1[jl]])
                P.op("act", lambda e, jl=jl: e.activation(out=t1[:, jl, :], in_=t1[:, jl, :], func=AF.Ln, bias=1.0),
                     reads=[t_t1[jl]], writes=[t_t1[jl]])
                for i in range(NT):
                    P.op("dve", lambda e, i=i, jl=jl: e.tensor_tensor_scan(
                        out=t2[:, i * 128:(i + 1) * 128], data0=ones[:, 0:128], data1=t1[:, jl, i * 128:(i + 1) * 128],
                        initial=0.0, op0=ALU.mult, op1=ALU.add),
                        reads=[t_t1[jl], t_ones], writes=[t_t2], signal=(i == NT - 1))
                P.op("act", lambda e, jl=jl: e.activation(out=t1[:, jl, :], in_=t2[:], func=AF.Exp, scale=-1.0 / 16.0),
                     reads=[t_t2], writes=[t_t1[jl]])
                P.op("act", lambda e: e.activation(out=t2[:], in_=t2[:], func=AF.Exp, scale=1.0 / 16.0),
                     reads=[t_t2], writes=[t_t2])
                P.op("dve", lambda e, jl=jl: e.tensor_copy(out=dec[:, jl, :], in_=t1[:, jl, 127:T:128]),
                     reads=[t_t1[jl]], writes=[t_dec])
                b = self.bank(2)
                for half in range(2):
                    for kc in range(KC):
                        P.op("pe", lambda e, half=half, kc=kc, b=b, jl=jl, v3=v3: e.matmul(
                            self.ps[:, b + half, :], lhsT=v3[:, kc, 256 + jl * 128:256 + (jl + 1) * 128],
                            rhs=self.xnT[:, kc, 2 + half * 512:2 + (half + 1) * 512], start=(kc == 0), stop=(kc == KC - 1)),
                            reads=[self.t_slots[sqk], self.t_xnT], writes=[self.t_ps[b + half]], signal=(kc == KC - 1))
                P.op("dve", lambda e, b=b, jl=jl: e.tensor_tensor(out=kiT[:, jl, :], in0=self.psflat(b, 2), in1=t2[:],
                                                                   op=ALU.mult),
                     reads=[self.t_ps[b], self.t_ps[b + 1], t_t2], writes=[t_kiT])
                P.op("dve", lambda e, jl=jl: e.tensor_tensor(
                    out=ktmp[:], in0=kiT[:, jl, :].rearrange("p (a b) -> p a b", a=8),
                    in1=dec[:, jl, :].unsqueeze(2).to_broadcast([128, 8, 128]), op=ALU.mult),
                    reads=[t_kiT, t_dec], writes=[t_ktmp])
                b = self.bank()
                pb = self.ps[:, b, :].bitcast(BF16)
                for i in range(NT):
                    P.op("pe", lambda e, i=i, pb=pb: e.transpose(out=pb[:, i * 128:(i + 1) * 128], in_=ktmp[:, i, :],
                                                                 identity=self.idb[:]),
                         reads=[t_ktmp, self.t_id], writes=[self.t_ps[b]], signal=(i == NT - 1))
                P.op("act", lambda e, pb=pb, jl=jl: e.copy(out=kte[:, :, jl * 128:(jl + 1) * 128],
                                                           in_=pb.rearrange("p (a b) -> p a b", a=8)),
                     reads=[self.t_ps[b]], writes=[t_kte])
            if DBG <= 3:
                P.barrier(); es.close(); return
            sv3 = self.slot3(sv, KC, 512)
            for i in range(NT):
                b = self.bank()
                for kc in range(KC):
                    P.op("pe", lambda e, i=i, kc=kc, b=b, sv3=sv3: e.matmul(
                        self.ps[:, b, :], lhsT=self.xnT[:, kc, 2 + i * 128:2 + (i + 1) * 128], rhs=sv3[:, kc, :],
                        start=(kc == 0), stop=(kc == KC - 1)),
                        reads=[self.t_slots[sv], self.t_xnT], writes=[self.t_ps[b]], signal=(kc == KC - 1))
                P.op("act", lambda e, i=i, b=b: e.copy(out=v_tm[:, i, :], in_=self.ps[:, b, :]),
                     reads=[self.t_ps[b]], writes=[t_v])
            sr = 1
            self.load_w(winr[:, :, 4096 + hd * 512:4096 + (hd + 1) * 512], sr, self.slot3(sr, KC, 512))
            for i in range(NT):
                for jl in range(2):
                    b = self.bank()
                    P.op("pe", lambda e, i=i, jl=jl, b=b: e.matmul(
                        self.ps[:, b, :], lhsT=kte[:, i, jl * 128:(jl + 1) * 128], rhs=v_tm[:, i, :], start=True, stop=True),
                        reads=[t_kte, t_v], writes=[self.t_ps[b]])
                    if i == 0:
                        P.op("dve", lambda e, jl=jl, b=b: e.tensor_copy(out=S[:, jl, :], in_=self.ps[:, b, :]),
                             reads=[self.t_ps[b]], writes=[t_S])
                    else:
                        P.op("dve", lambda e, i=i, jl=jl, b=b: e.scalar_tensor_tensor(
                            out=S[:, jl, :], in0=S[:, jl, :], scalar=dec[:, jl, i:i + 1], in1=self.ps[:, b, :],
                            op0=ALU.mult, op1=ALU.add),
                            reads=[self.t_ps[b], t_S, t_dec], writes=[t_S])
            if DBG <= 4:
                P.barrier(); es.close(); return
            for jl in range(2):
                P.op("dve", lambda e, jl=jl: e.tensor_reduce(out=Dl[:, jl, 0:1], in_=dec[:, jl, :], axis=mybir.AxisListType.X,
                                                             op=ALU.mult),
                     reads=[t_dec], writes=[t_Dl])
            gs = gsrc[hd]
            P.dma("sp", lambda e, gs=gs: e.dma_start(out=gs.rearrange("(a p) c -> p a c", p=128)[:, :, 0:512], in_=S[:]),
                  reads=[t_S], writes=[t_gsrc[hd]])
            P.dma("sp", lambda e, gs=gs: e.dma_start(out=gs.rearrange("(a p) c -> p a c", p=128)[:, :, 512:520], in_=Dl[:]),
                  reads=[t_Dl], writes=[t_gsrc[hd]])
            self.allgather(gsrc[hd], gdst[hd], t_gsrc[hd], t_gdst[hd])
            for jl in range(2):
                b = self.bank(2)
                for half in range(2):
                    for kc in range(KC):
                        P.op("pe", lambda e, half=half, kc=kc, b=b, jl=jl, v3=v3: e.matmul(
                            self.ps[:, b + half, :], lhsT=v3[:, kc, jl * 128:(jl + 1) * 128],
                            rhs=self.xnT[:, kc, 2 + half * 512:2 + (half + 1) * 512], start=(kc == 0), stop=(kc == KC - 1)),
                            reads=[self.t_slots[sqk], self.t_xnT], writes=[self.t_ps[b + half]], signal=(kc == KC - 1))
                P.op("dve", lambda e, b=b, jl=jl: e.scalar_tensor_tensor(
                    out=qdT[:, jl, :], in0=self.psflat(b, 2), scalar=0.0625, in1=t1[:, jl, :], op0=ALU.mult, op1=ALU.mult),
                    reads=[self.t_ps[b], self.t_ps[b + 1], t_t1[jl]], writes=[t_qdT])
            if DBG <= 5:
                P.barrier(); es.close(); return
            sr3 = self.slot3(sr, KC, 512)
            for i in range(NT):
                b = self.bank()
                for kc in range(KC):
                    P.op("pe", lambda e, i=i, kc=kc, b=b, sr3=sr3: e.matmul(
                        self.ps[:, b, :], lhsT=self.xnT[:, kc, 2 + i * 128:2 + (i + 1) * 128], rhs=sr3[:, kc, :],
                        start=(kc == 0), stop=(kc == KC - 1)),
                        reads=[self.t_slots[sr], self.t_xnT], writes=[self.t_ps[b]], signal=(kc == KC - 1))
                gt = gtmp[i % 2]
                P.op("act", lambda e, b=b, gt=gt: e.activation(out=gt[:], in_=self.ps[:, b, :], func=AF.Silu),
                     reads=[self.t_ps[b]], writes=[t_gtmp[i % 2]])
                P.op("dve", lambda e, i=i, gt=gt: e.tensor_tensor(out=gate[:, i, :], in0=gt[:], in1=hnb[:], op=ALU.mult),
                     reads=[t_gtmp[i % 2], t_hnb], writes=[t_gate])
            if DBG == 6:
                P.barrier(); es.close(); return
            so = 0
            self.load_w(w_out[hd * 512:(hd + 1) * 512, :].rearrange("(a p) n -> p a n", p=128), so, self.slot3(so, 4, D))
            if DBG == 62:
                P.barrier(); es.close(); return
            P.op("dve", lambda e: e.memset(S[:], 0.0), writes=[t_S])
            self.dyn_dma(gstage[hd][:, :], gdst_h[hd], "gb", 256 * 520, 0, [[520, 768], [1, 520]],
                         reads=[t_gdst[hd]], writes=[t_gstage[hd]])
            for ri in range(3):
                Rr = R[ri % 2]
                tR = t_R[ri % 2]
                for jl in range(2):
                    P.dma("sp", lambda e, ri=ri, jl=jl, Rr=Rr, hd=hd: e.dma_start(
                        out=Rr[:, jl, :], in_=gstage[hd][(ri * 256 + jl * 128):(ri * 256 + jl * 128) + 128, :]),
                        reads=[t_gstage[hd]], writes=[tR[jl]])
                for jl in range(2):
                    P.op("dve", lambda e, Rr=Rr, jl=jl: e.tensor_scalar(out=Rr[:, jl, 513:514], in0=Rr[:, jl, 512:513], scalar1=-1.0,
                                                                        scalar2=None, op0=ALU.add), reads=[tR[jl]], writes=[tR[jl]])
                    P.op("dve", lambda e, jl=jl, Rr=Rr: e.scalar_tensor_tensor(
                        out=ftmp[:], in0=S[:, jl, :], scalar=Rr[:, jl, 513:514], in1=Rr[:, jl, 0:512],
                        op0=ALU.mult, op1=ALU.add), reads=[t_S, tR[jl]], writes=[t_ftmp])
                    P.op("dve", lambda e, jl=jl, ri=ri: e.scalar_tensor_tensor(
                        out=S[:, jl, :], in0=ftmp[:], scalar=self.cst[:, C_SEL + ri:C_SEL + ri + 1], in1=S[:, jl, :],
                        op0=ALU.mult, op1=ALU.add), reads=[t_ftmp, t_S, self.t_cst], writes=[t_S])
            P.op("act", lambda e: e.copy(out=Sbf[:], in_=S[:]), reads=[t_S], writes=[t_Sbf])
            if DBG in (7, 64):
                P.barrier(); es.close(); return
            so3 = self.slot3(so, 4, D)
            for i in range(NT):
                tk = slice(i * 128, (i + 1) * 128)
                b = self.bank()
                for jl in range(2):
                    P.op("pe", lambda e, jl=jl, b=b, tk=tk: e.matmul(
                        self.ps[:, b, 0:128], lhsT=kiT[:, jl, tk], rhs=qdT[:, jl, tk], start=(jl == 0), stop=(jl == 1)),
                        reads=[t_kiT, t_qdT], writes=[self.t_ps[b]], signal=(jl == 1))
                pt = pT[i % 2]
                P.op("dve", lambda e, b=b, pt=pt: e.tensor_tensor(out=pt[:], in0=self.ps[:, b, 0:128], in1=cmask[:], op=ALU.mult),
                     reads=[self.t_ps[b], t_cm], writes=[t_pT[i % 2]])
                bo = self.bank()
                P.op("pe", lambda e, bo=bo, pt=pt, i=i: e.matmul(self.ps[:, bo, :], lhsT=pt[:], rhs=v_tm[:, i, :],
                                                                 start=True, stop=False),
                     reads=[t_pT[i % 2], t_v], writes=[self.t_ps[bo]], signal=False)
                for jl in range(2):
                    P.op("pe", lambda e, bo=bo, jl=jl, tk=tk: e.matmul(self.ps[:, bo, :], lhsT=qdT[:, jl, tk], rhs=Sbf[:, jl, :],
                                                                      start=False, stop=(jl == 1)),
                         reads=[t_qdT, t_Sbf], writes=[self.t_ps[bo]], signal=(jl == 1))
                if i < NT - 1:
                    for jl in range(2):
                        bu = self.bank()
                        P.op("pe", lambda e, i=i, jl=jl, bu=bu: e.matmul(
                            self.ps[:, bu, :], lhsT=kte[:, i, jl * 128:(jl + 1) * 128], rhs=v_tm[:, i, :], start=True, stop=True),
                            reads=[t_kte, t_v], writes=[self.t_ps[bu]])
                        P.op("dve", lambda e, i=i, jl=jl, bu=bu: e.scalar_tensor_tensor(
                            out=S[:, jl, :], in0=S[:, jl, :], scalar=dec[:, jl, i:i + 1], in1=self.ps[:, bu, :],
                            op0=ALU.mult, op1=ALU.add), reads=[self.t_ps[bu], t_S, t_dec], writes=[t_S])
                    P.op("act", lambda e: e.copy(out=Sbf[:], in_=S[:]), reads=[t_S], writes=[t_Sbf])
                ogt = og[i % 2]
                P.op("act", lambda e, bo=bo, ogt=ogt: e.activation(out=ogt[:], in_=self.ps[:, bo, :], func=AF.Square,
                                                                   accum_out=st2[:, 0:1]),
                     reads=[self.t_ps[bo]], writes=[t_og[i % 2], t_st2])
                P.op("dve", lambda e: e.tensor_scalar(out=st2[:, 1:2], in0=st2[:, 0:1], scalar1=1.0 / 512, scalar2=EPS,
                                                      op0=ALU.mult, op1=ALU.add), reads=[t_st2], writes=[t_st2])
                P.op("act", lambda e: e.activation(out=st2[:, 2:3], in_=st2[:, 1:2], func=AF.Sqrt), reads=[t_st2], writes=[t_st2])
                P.op("dve", lambda e: e.reciprocal(out=st2[:, 3:4], in_=st2[:, 2:3]), reads=[t_st2], writes=[t_st2])
                P.op("dve", lambda e, bo=bo, ogt=ogt, i=i: e.scalar_tensor_tensor(
                    out=ogt[:], in0=self.ps[:, bo, :], scalar=st2[:, 3:4], in1=gate[:, i, :], op0=ALU.mult, op1=ALU.mult),
                    reads=[self.t_ps[bo], t_st2, t_gate], writes=[t_og[i % 2]])
                bt = self.bank()
                pb = self.ps[:, bt, :].bitcast(BF16)
                for vc in range(4):
                    P.op("pe", lambda e, vc=vc, pb=pb, ogt=ogt: e.transpose(out=pb[:, vc * 128:(vc + 1) * 128],
                                                                          in_=ogt[:, vc * 128:(vc + 1) * 128], identity=self.idb[:]),
                         reads=[t_og[i % 2], self.t_id], writes=[self.t_ps[bt]], signal=(vc == 3))
                ogTt = ogT[i % 2]
                P.op("act", lambda e, pb=pb, ogTt=ogTt: e.copy(out=ogTt[:], in_=pb[:, 0:512].rearrange("p (a b) -> p a b", a=4)),
                     reads=[self.t_ps[bt]], writes=[t_ogT[i % 2]])
                for np_ in range(4):
                    b = self.bank()
                    for vc in range(4):
                        P.op("pe", lambda e, np_=np_, vc=vc, b=b, so3=so3, ogTt=ogTt: e.matmul(
                            self.ps[:, b, :], lhsT=ogTt[:, vc, :], rhs=so3[:, vc, np_ * 512:(np_ + 1) * 512],
                            start=(vc == 0), stop=(vc == 3)),
                            reads=[t_ogT[i % 2], self.t_slots[so]], writes=[self.t_ps[b]], signal=(vc == 3))
                    P.op("dve", lambda e, i=i, np_=np_, b=b: e.tensor_tensor(
                        out=self.h[:, i, np_ * 512:(np_ + 1) * 512], in0=self.ps[:, b, :], in1=self.h[:, i, np_ * 512:(np_ + 1) * 512],
                        op=ALU.add), reads=[self.t_ps[b], self.t_h[i]], writes=[self.t_h[i]])
        P.barrier()
        es.close()


    def ffn(self, l):
        P = self.P
        nc = self.nc
        w_up = self.inp("ffn_w_up%d" % l, [D, 2 * DFF])
        w_down = self.inp("ffn_w_down%d" % l, [DFF, D])
        wupr = w_up.rearrange("(kc p) n -> p kc n", p=128)
        es = ExitStack()
        self.es.enter_context(es)

        def sb(name, shape, dt):
            return es.enter_context(nc.sbuf_tensor("f%d_" % l + name, list(shape), dt))
        NS = 8
        wsl = [sb("ws%d" % i, [128, 4096], BF16) for i in range(NS)]
        t_wsl = [Tile("ws%d" % i) for i in range(NS)]
        free = list(range(NS))
        hid = [sb("hid%d" % i, [128, 4, T], BF16) for i in range(2)]
        t_hid = [Tile("hid0"), Tile("hid1")]
        acc = sb("acc", [128, T], F32); t_acc = Tile("acc")
        hsrc = nc.dram_tensor("hsrc%d" % l, [128, 32], BF16).ap()
        hdst_h = nc.dram_tensor("hdst%d" % l, [8 * 128, 32], BF16, addr_space="Shared")
        hdst = hdst_h.ap()
        t_hsrc = Tile("hsrc"); t_hdst = Tile("hdst")
        UB, GB = 0, 2
        cnt = [0]

        def load_up(col0):
            s_ = free.pop(0)
            v = wsl[s_][:, :].rearrange("p (a b) -> p a b", a=KC)
            P.dma("pool", lambda e: e.dma_start(out=v, in_=wupr[:, :, col0:col0 + 256]), writes=[t_wsl[s_]])
            return s_

        def load_down(row0):
            s_ = free.pop(0)
            v = wsl[s_][:, :].rearrange("p (a b) -> p a b", a=2)
            src = w_down[row0:row0 + 256, :].rearrange("(a p) n -> p a n", p=128)
            for c0 in range(0, D, 512):
                P.dma("pool", lambda e, c0=c0: e.dma_start(out=v[:, :, c0:c0 + 512], in_=src[:, :, c0:c0 + 512]),
                      writes=[t_wsl[s_]])
            return s_

        def down_groups(fp, dsl, i_list):
            hb = hid[fp % 2]
            for (i, np_) in i_list:
                if True:
                    b = 5 + (cnt[0] % 3)
                    cnt[0] += 1
                    for fc in range(4):
                        s_ = dsl[fc // 2]
                        v = wsl[s_][:, :].rearrange("p (a b) -> p a b", a=2)
                        P.op("pe", lambda e, i=i, np_=np_, fc=fc, b=b, v=v, hb=hb: e.matmul(
                            self.ps[:, b, :], lhsT=hb[:, fc, i * 128:(i + 1) * 128], rhs=v[:, fc % 2, np_ * 512:(np_ + 1) * 512],
                            start=(fc == 0), stop=(fc == 3)),
                            reads=[t_hid[fp % 2], t_wsl[s_]], writes=[self.t_ps[b]], signal=(fc == 3))
                    P.op("dve", lambda e, i=i, np_=np_, b=b: e.tensor_tensor(
                        out=self.h[:, i, np_ * 512:(np_ + 1) * 512], in0=self.ps[:, b, :],
                        in1=self.h[:, i, np_ * 512:(np_ + 1) * 512], op=ALU.add),
                        reads=[self.t_ps[b], self.t_h[i]], writes=[self.t_h[i]])

        U = {0: [load_up(0), load_up(DFF), load_up(256), load_up(DFF + 256)]}
        Dn = {}
        P.dma("sp", lambda e: e.dma_start(out=hsrc.rearrange("p (a b) -> p a b", b=2), in_=self.xnT[:, :, T:T + 2]),
              reads=[self.t_xnT], writes=[t_hsrc])
        self.allgather(hsrc, hdst, t_hsrc, t_hdst)
        self.dyn_dma(self.xnT[:, :, 0:2], hdst_h, "rm1", 128 * 32, 0, [[32, 128], [2, KC], [1, 2]],
                     reads=[t_hdst], writes=[self.t_halo])
        P.op("dve", lambda e: e.tensor_scalar(out=self.xnT[:, :, 0:2], in0=self.xnT[:, :, 0:2],
                                              scalar1=self.cst[:, C_HALO:C_HALO + 1], scalar2=None, op0=ALU.mult),
             reads=[self.t_halo, self.t_cst], writes=[self.t_halo])
        for fp in range(NFP):
            Dn[fp] = [load_down(fp * 512), load_down(fp * 512 + 256)]
            for fc in range(4):
                fcg = fp * 4 + fc
                su = U[fp][(fc // 2) * 2]
                sg = U[fp][(fc // 2) * 2 + 1]
                su3 = wsl[su][:, :].rearrange("p (a b) -> p a b", a=KC)
                sg3 = wsl[sg][:, :].rearrange("p (a b) -> p a b", a=KC)
                c_ = (fc % 2) * 128
                for half in range(2):
                    for kc in range(KC):
                        P.op("pe", lambda e, half=half, kc=kc, c_=c_, su3=su3: e.matmul(
                            self.ps[:, UB + half, :], lhsT=su3[:, kc, c_:c_ + 128],
                            rhs=self.xnT[:, kc, 2 + half * 512:2 + (half + 1) * 512], start=(kc == 0), stop=(kc == KC - 1)),
                            reads=[t_wsl[su], self.t_xnT], writes=[self.t_ps[UB + half]], signal=(kc == KC - 1))
                if fp > 0:
                    down_groups(fp - 1, Dn[fp - 1], [(2 * fc, 0), (2 * fc, 1), (2 * fc, 2), (2 * fc, 3), (2 * fc + 1, 0)])
                for piece, (c0, c1) in enumerate(((0, 512), (512, 1024), (1024, 1026))):
                    rd = [t_wsl[sg], self.t_xnT] + ([self.t_halo] if piece == 0 else [])
                    for kc in range(KC):
                        P.op("pe", lambda e, piece=piece, c0=c0, c1=c1, kc=kc, c_=c_, sg3=sg3: e.matmul(
                            self.ps[:, GB + piece, 0:c1 - c0], lhsT=sg3[:, kc, c_:c_ + 128],
                            rhs=self.xnT[:, kc, c0:c1], start=(kc == 0), stop=(kc == KC - 1)),
                            reads=rd, writes=[self.t_ps[GB + piece]], signal=(kc == KC - 1))
                gflat = self.psflat(GB, 3)
                gt = [self.t_ps[GB], self.t_ps[GB + 1], self.t_ps[GB + 2]]
                cw = [C_CONVW + (l * 3 + tap) * 44 + fcg for tap in range(3)]
                cb = C_CONVB + l * 44 + fcg
                P.op("act", lambda e, gflat=gflat, cw=cw, cb=cb: e.activation(
                    out=acc[:], in_=gflat[:, 2:T + 2], func=AF.Identity, bias=self.cst[:, cb:cb + 1], scale=self.cst[:, cw[2]:cw[2] + 1]),
                    reads=gt + [self.t_cst], writes=[t_acc])
                P.op("dve", lambda e, gflat=gflat, cw=cw: e.scalar_tensor_tensor(
                    out=acc[:], in0=gflat[:, 1:T + 1], scalar=self.cst[:, cw[1]:cw[1] + 1], in1=acc[:], op0=ALU.mult, op1=ALU.add),
                    reads=gt + [self.t_cst, t_acc], writes=[t_acc])
                P.op("dve", lambda e, gflat=gflat, cw=cw: e.scalar_tensor_tensor(
                    out=acc[:], in0=gflat[:, 0:T], scalar=self.cst[:, cw[0]:cw[0] + 1], in1=acc[:], op0=ALU.mult, op1=ALU.add),
                    reads=gt + [self.t_cst, t_acc], writes=[t_acc])
                P.op("act", lambda e: e.activation(out=acc[:], in_=acc[:], func=AF.Gelu), reads=[t_acc], writes=[t_acc])
                hb = hid[fp % 2]
                P.op("dve", lambda e, hb=hb, fc=fc: e.tensor_tensor(out=hb[:, fc, :], in0=self.psflat(UB, 2), in1=acc[:], op=ALU.mult),
                     reads=[self.t_ps[UB], self.t_ps[UB + 1], t_acc], writes=[t_hid[fp % 2]])
                if fp > 0:
                    down_groups(fp - 1, Dn[fp - 1], [(2 * fc + 1, 1), (2 * fc + 1, 2), (2 * fc + 1, 3)])
                if fc == 1:
                    free.extend(U[fp][0:2])
                    if fp + 1 < NFP:
                        U[fp + 1] = [load_up((fp + 1) * 512), load_up(DFF + (fp + 1) * 512)]
                if fc == 3:
                    free.extend(U[fp][2:4])
                    if fp > 0:
                        free.extend(Dn[fp - 1])
                    if fp + 1 < NFP:
                        U[fp + 1] += [load_up((fp + 1) * 512 + 256), load_up(DFF + (fp + 1) * 512 + 256)]
        down_groups(NFP - 1, Dn[NFP - 1], [(i, np_) for i in range(NT) for np_ in range(4)])
        self._bank = 0
        P.barrier()
        es.close()


    def kv(self):
        P = self.P
        nc = self.nc
        w_kv = self.inp("w_kv", [D, 2 * D])
        wkvr = w_kv.rearrange("(kc p) n -> p kc n", p=128)
        es = ExitStack()
        self.es.enter_context(es)
        self.alloc_slots(es, 2)

        def sb(name, shape, dt):
            return es.enter_context(nc.sbuf_tensor("kv_" + name, list(shape), dt))
        kst = [sb("kst%d" % i, [128, T], BF16) for i in range(2)]; t_kst = [Tile("kst0"), Tile("kst1")]
        vst = [sb("vst%d" % i, [128, 512], BF16) for i in range(2)]; t_vst = [Tile("vst0"), Tile("vst1")]
        self.ksrc = nc.dram_tensor("ksrc", [D, T], BF16).ap()
        self.vsrc = nc.dram_tensor("vsrc", [T, D], BF16).ap()
        kdst_h = nc.dram_tensor("kdst", [8 * D, T], BF16, addr_space="Shared")
        vdst_h = nc.dram_tensor("vdst", [8 * T, D], BF16, addr_space="Shared")
        self.kstage = nc.dram_tensor("kstage", [2 * D, T], BF16).ap()
        self.vstage = nc.dram_tensor("vstage", [2 * T, D], BF16).ap()
        self.t_ksrc = Tile("ksrc"); self.t_vsrc = Tile("vsrc")
        t_kdst = Tile("kdst"); t_vdst = Tile("vdst")
        self.t_kstage = Tile("kstage"); self.t_vstage = Tile("vstage")
        self.hspill = nc.dram_tensor("hspill", [T, D], F32).ap()
        self.t_spill = Tile("hspill")
        for i in range(NT):
            P.dma("sp", lambda e, i=i: e.dma_start(out=self.hspill[i * 128:(i + 1) * 128, :], in_=self.h[:, i, :]),
                  reads=[self.t_h[i]], writes=[self.t_spill])
        n = 0
        for hg in range(4):
            s_ = self.next_slot()
            s3 = self.slot3(s_, KC, 512)
            self.load_w(wkvr[:, :, hg * 512:(hg + 1) * 512], s_, s3)
            for hl in range(4):
                hh = hg * 4 + hl
                b = self.bank(2)
                for half in range(2):
                    for kc in range(KC):
                        P.op("pe", lambda e, half=half, kc=kc, b=b, hl=hl, s3=s3: e.matmul(
                            self.ps[:, b + half, :], lhsT=s3[:, kc, hl * 128:(hl + 1) * 128],
                            rhs=self.xnT[:, kc, 2 + half * 512:2 + (half + 1) * 512], start=(kc == 0), stop=(kc == KC - 1)),
                            reads=[self.t_slots[s_], self.t_xnT], writes=[self.t_ps[b + half]], signal=(kc == KC - 1))
                ks = kst[n % 2]; tk = t_kst[n % 2]; n += 1
                P.op("act", lambda e, b=b, ks=ks: e.copy(out=ks[:], in_=self.psflat(b, 2)),
                     reads=[self.t_ps[b], self.t_ps[b + 1]], writes=[tk])
                P.dma("sp", lambda e, hh=hh, ks=ks: e.dma_start(out=self.ksrc[hh * 128:(hh + 1) * 128, :], in_=ks[:]),
                      reads=[tk], writes=[self.t_ksrc])
        n = 0
        for np_ in range(4):
            s_ = self.next_slot()
            s3 = self.slot3(s_, KC, 512)
            self.load_w(wkvr[:, :, D + np_ * 512:D + (np_ + 1) * 512], s_, s3)
            for i in range(NT):
                b = self.bank()
                for kc in range(KC):
                    P.op("pe", lambda e, i=i, kc=kc, b=b, s3=s3: e.matmul(
                        self.ps[:, b, :], lhsT=self.xnT[:, kc, 2 + i * 128:2 + (i + 1) * 128], rhs=s3[:, kc, :],
                        start=(kc == 0), stop=(kc == KC - 1)),
                        reads=[self.t_slots[s_], self.t_xnT], writes=[self.t_ps[b]], signal=(kc == KC - 1))
                vs = vst[n % 2]; tv = t_vst[n % 2]; n += 1
                P.op("act", lambda e, b=b, vs=vs: e.copy(out=vs[:], in_=self.ps[:, b, :]), reads=[self.t_ps[b]], writes=[tv])
                P.dma("sp", lambda e, i=i, np_=np_, vs=vs: e.dma_start(
                    out=self.vsrc[i * 128:(i + 1) * 128, np_ * 512:(np_ + 1) * 512], in_=vs[:]),
                    reads=[tv], writes=[self.t_vsrc])
        P.barrier()
        es.close()
        self.allgather(self.ksrc, kdst_h.ap(), self.t_ksrc, t_kdst)
        self.allgather(self.vsrc, vdst_h.ap(), self.t_vsrc, t_vdst)
        for ri, nm in enumerate(("rm2", "rm1")):
            self.dyn_dma(self.kstage[ri * D:(ri + 1) * D, :], kdst_h, nm, D * T, 0, [[T, D], [1, T]],
                         reads=[t_kdst], writes=[self.t_kstage])
            self.dyn_dma(self.vstage[ri * T:(ri + 1) * T, :], vdst_h, nm, T * D, 0, [[D, T], [1, D]],
                         reads=[t_vdst], writes=[self.t_vstage])

    def attn(self):
        P = self.P
        nc = self.nc
        w_q = self.inp("dsa_w_q", [D, 3 * D])
        w_o = self.inp("dsa_w_out", [D, D])
        jt_d = self.inp("jtile", [128, 256])
        wqr = w_q.rearrange("(kc p) n -> p kc n", p=128)
        wor = w_o.rearrange("(kc p) n -> p kc n", p=128)
        hspill = self.hspill
        t_spill = self.t_spill
        P.barrier()
        es = ExitStack()
        self.es.enter_context(es)
        self.alloc_slots(es, 2)

        def sb(name, shape, dt):
            return es.enter_context(nc.sbuf_tensor("at_" + name, list(shape), dt))
        hflat = self.h[:].rearrange("p a b -> p (a b)")
        numden = hflat[:, 0:8192].rearrange("p (h a t) -> p h a t", h=4, a=2)
        kTg = hflat[:, 8192:14336].bitcast(BF16).rearrange("p (h t) -> p h t", h=4)
        qTg = hflat[:, 14336:16384].bitcast(BF16).rearrange("p (h t) -> p h t", h=4)
        t_nd = [Tile("nd%d" % i) for i in range(4)]
        t_kTg = Tile("kTg"); t_qTg = Tile("qTg")
        attnT = sb("attnT", [128, 16, T], BF16); t_attnT = Tile("attnT")
        vbuf = [sb("vbuf%d" % i, [128, 12, 512], BF16) for i in range(2)]; t_vbuf = [Tile("vbuf0"), Tile("vbuf1")]
        NB = 4
        sc = [sb("sc%d" % i, [128, 2, 128], F32) for i in range(NB)]; t_sc = [Tile("sc%d" % i) for i in range(NB)]
        pT = [sb("pT%d" % i, [128, 2, 128], BF16) for i in range(NB)]; t_pT = [Tile("pT%d" % i) for i in range(NB)]
        jt = sb("jt", [128, 2, 128], F32); t_jt = Tile("jt")
        onesb = sb("onesb", [128, 128], BF16); t_ones = Tile("onesb")
        rden = sb("rden", [128, T], F32); t_rden = Tile("rden")
        P.dma("sp", lambda e: e.dma_start(out=jt[:], in_=jt_d.rearrange("p (a b) -> p a b", a=2)), writes=[t_jt])
        P.op("dve", lambda e: e.memset(onesb[:], 1.0), writes=[t_ones])
        SQ = float(np.sqrt(128.0))
        slopes = [2.0 ** (-0.5 * (hh + 1)) for hh in range(16)]
        nblk = [0]
        nvb = [0]

        pend = []

        def stage_a(hl, hh, d, Q, kslices, vtiles, qslice, kbcols, vb, first):
            k_ = nblk[0] % NB
            nblk[0] += 1
            b = self.bank()
            for t_i, (ksl, nk) in enumerate(kslices):
                P.op("pe", lambda e, t_i=t_i, ksl=ksl, nk=nk, b=b: e.matmul(
                    self.ps[0:nk, b, t_i * 128:t_i * 128 + Q], lhsT=kTg[:, hl, ksl], rhs=qTg[:, hl, qslice], start=True, stop=True),
                    reads=[t_kTg, t_qTg], writes=[self.t_ps[b]], signal=(t_i == 1))
            coef = -slopes[hh] * d * SQ
            scb = sc[k_]; ptb = pT[k_]
            full = (Q == 128 and kslices[0][1] == 128 and kslices[1][1] == 128)
            if full:
                P.op("dve", lambda e, b=b, scb=scb: e.scalar_tensor_tensor(
                    out=scb[:, :, :], in0=jt[:, :, :], scalar=coef, in1=self.ps[:, b, 0:256].rearrange("p (a b) -> p a b", a=2),
                    op0=ALU.mult, op1=ALU.add), reads=[self.t_ps[b], t_jt], writes=[t_sc[k_]])
            else:
                for t_i, (ksl, nk) in enumerate(kslices):
                    P.op("dve", lambda e, t_i=t_i, nk=nk, b=b, scb=scb: e.scalar_tensor_tensor(
                        out=scb[0:nk, t_i, 0:Q], in0=jt[0:nk, t_i, 0:Q], scalar=coef, in1=self.ps[0:nk, b, t_i * 128:t_i * 128 + Q],
                        op0=ALU.mult, op1=ALU.add), reads=[self.t_ps[b], t_jt], writes=[t_sc[k_]])
            if full and kbcols[0] is None and kbcols[1] is None:
                P.op("act", lambda e, scb=scb, ptb=ptb: e.activation(out=ptb[:, :, :], in_=scb[:, :, :], func=AF.Exp, scale=1.0 / SQ),
                     reads=[t_sc[k_]], writes=[t_pT[k_]])
            else:
                for t_i, (ksl, nk) in enumerate(kslices):
                    kb = kbcols[t_i]
                    if kb is None:
                        P.op("act", lambda e, t_i=t_i, nk=nk, scb=scb, ptb=ptb: e.activation(
                            out=ptb[0:nk, t_i, 0:Q], in_=scb[0:nk, t_i, 0:Q], func=AF.Exp, scale=1.0 / SQ),
                            reads=[t_sc[k_]], writes=[t_pT[k_]])
                    else:
                        P.op("act", lambda e, t_i=t_i, nk=nk, scb=scb, ptb=ptb, kb=kb: e.activation(
                            out=ptb[0:nk, t_i, 0:Q], in_=scb[0:nk, t_i, 0:Q], func=AF.Exp, scale=1.0 / SQ,
                            bias=self.cst[0:nk, kb:kb + 1]), reads=[t_sc[k_], self.t_cst], writes=[t_pT[k_]])
            return (hl, Q, kslices, vtiles, qslice, vb, first, k_)

        def stage_b(ctx):
            hl, Q, kslices, vtiles, qslice, vb, first, k_ = ctx
            ptb = pT[k_]
            bo = self.bank()
            for t_i, (ksl, nk) in enumerate(kslices):
                P.op("pe", lambda e, t_i=t_i, nk=nk, bo=bo, ptb=ptb: e.matmul(
                    self.ps[:, bo, 0:Q], lhsT=vbuf[vb][0:nk, vtiles[t_i], hl * 128:(hl + 1) * 128], rhs=ptb[0:nk, t_i, 0:Q],
                    start=(t_i == 0), stop=(t_i == 1)), reads=[t_vbuf[vb], t_pT[k_]], writes=[self.t_ps[bo]], signal=False)
            for t_i, (ksl, nk) in enumerate(kslices):
                P.op("pe", lambda e, t_i=t_i, nk=nk, bo=bo, ptb=ptb: e.matmul(
                    self.ps[:, bo, 128:128 + Q], lhsT=onesb[0:nk, :], rhs=ptb[0:nk, t_i, 0:Q],
                    start=(t_i == 0), stop=(t_i == 1)), reads=[t_ones, t_pT[k_]], writes=[self.t_ps[bo]], signal=(t_i == 1))
            src = self.ps[:, bo, 0:256].rearrange("p (a b) -> p a b", a=2)[:, :, 0:Q]
            dst = numden[:, hl, :, qslice]
            if first:
                P.op("dve", lambda e, src=src, dst=dst: e.tensor_copy(out=dst, in_=src), reads=[self.t_ps[bo]], writes=[t_nd[hl]])
            else:
                P.op("dve", lambda e, src=src, dst=dst: e.tensor_tensor(out=dst, in0=src, in1=dst, op=ALU.add),
                     reads=[self.t_ps[bo], t_nd[hl]], writes=[t_nd[hl]])

        def block(*args):
            ctx = stage_a(*args)
            pend.append(ctx)
            if len(pend) > 1:
                stage_b(pend.pop(0))

        def flush():
            while pend:
                stage_b(pend.pop(0))

        def vload(vb, p0, n, tile0, ntile, src):
            if ntile == 1:
                P.dma("sp", lambda e: e.dma_start(out=vbuf[vb][p0:p0 + n, tile0, :], in_=src), writes=[t_vbuf[vb]])
            else:
                P.dma("sp", lambda e: e.dma_start(out=vbuf[vb][p0:p0 + n, tile0:tile0 + ntile, :],
                                                   in_=src.rearrange("(a p) c -> p a c", p=n)), writes=[t_vbuf[vb]])

        for hg in range(4):
            c0, c1 = hg * 512, (hg + 1) * 512
            for ri in range(2):
                P.dma("sp", lambda e, ri=ri, c0=c0, c1=c1: e.dma_start(
                    out=kTg[:, :, ri * T:(ri + 1) * T],
                    in_=self.kstage[ri * D + c0:ri * D + c1, :].rearrange("(a p) t -> p a t", p=128)),
                    reads=[self.t_kstage], writes=[t_kTg])
            P.dma("sp", lambda e, c0=c0, c1=c1: e.dma_start(out=kTg[:, :, 2 * T:3 * T],
                                               in_=self.ksrc[c0:c1, :].rearrange("(a p) t -> p a t", p=128)),
                  reads=[self.t_ksrc], writes=[t_kTg])
            for g, d in enumerate((1, 4, 16)):
                s_ = self.next_slot()
                s3 = self.slot3(s_, KC, 512)
                self.load_w(wqr[:, :, g * D + c0:g * D + c1], s_, s3)
                for hl in range(4):
                    b = self.bank(2)
                    for half in range(2):
                        for kc in range(KC):
                            P.op("pe", lambda e, half=half, kc=kc, b=b, hl=hl, s3=s3: e.matmul(
                                self.ps[:, b + half, :], lhsT=s3[:, kc, hl * 128:(hl + 1) * 128],
                                rhs=self.xnT[:, kc, 2 + half * 512:2 + (half + 1) * 512], start=(kc == 0), stop=(kc == KC - 1)),
                                reads=[self.t_slots[s_], self.t_xnT], writes=[self.t_ps[b + half]], signal=(kc == KC - 1))
                    P.op("act", lambda e, b=b, hl=hl: e.copy(out=qTg[:, hl, :], in_=self.psflat(b, 2)),
                         reads=[self.t_ps[b], self.t_ps[b + 1]], writes=[t_qTg])
                vs_own = self.vsrc[:, c0:c1]
                vs_m1 = self.vstage[T:2 * T, c0:c1]
                vs_m2 = self.vstage[0:T, c0:c1]
                rd = [self.t_vsrc, self.t_vstage]
                if d == 1:
                    vb = nvb[0] % 2; nvb[0] += 1
                    P.wait_all("sp", [])
                    P.dma("sp", lambda e, vb=vb, vs_m1=vs_m1: e.dma_start(out=vbuf[vb][:, 0, :], in_=vs_m1[896:1024, :]), reads=rd, writes=[t_vbuf[vb]])
                    P.dma("sp", lambda e, vb=vb, vs_own=vs_own: e.dma_start(out=vbuf[vb][:, 1:9, :], in_=vs_own.rearrange("(a p) c -> p a c", p=128)),
                          reads=rd, writes=[t_vbuf[vb]])
                    for hl in range(4):
                        for n in range(8):
                            block(hl, hg * 4 + hl, 1, 128,
                                  [(slice(1920 + 128 * n, 2048 + 128 * n), 128), (slice(2048 + 128 * n, 2176 + 128 * n), 128)],
                                  [n, n + 1], slice(128 * n, 128 * n + 128), [C_KB + 0 if n == 0 else None, None], vb, True)
                elif d == 4:
                    vb = nvb[0] % 2; nvb[0] += 1
                    for r in range(4):
                        P.dma("sp", lambda e, vb=vb, r=r, vs_m1=vs_m1: e.dma_start(out=vbuf[vb][:, 3 * r, :], in_=vs_m1[512 + r:1024:4, :]),
                              reads=rd, writes=[t_vbuf[vb]])
                        P.dma("sp", lambda e, vb=vb, r=r, vs_own=vs_own: e.dma_start(out=vbuf[vb][:, 3 * r + 1:3 * r + 3, :],
                                                                       in_=vs_own[r:1024:4, :].rearrange("(a p) c -> p a c", p=128)),
                              reads=rd, writes=[t_vbuf[vb]])
                    for hl in range(4):
                        for r in range(4):
                            for m in range(2):
                                kp = 2048 + r + 512 * (m - 1)
                                kc_ = 2048 + r + 512 * m
                                block(hl, hg * 4 + hl, 4, 128,
                                      [(slice(kp, kp + 509, 4), 128), (slice(kc_, kc_ + 509, 4), 128)],
                                      [3 * r + m, 3 * r + m + 1], slice(r + 512 * m, r + 512 * m + 509, 4),
                                      [C_KB + 1 if m == 0 else None, None], vb, False)
                else:
                    for r0 in range(0, 16, 4):
                        vb = nvb[0] % 2; nvb[0] += 1
                        for rr in range(4):
                            r = r0 + rr
                            P.dma("sp", lambda e, vb=vb, r=r, rr=rr, vs_m2=vs_m2: e.dma_start(out=vbuf[vb][0:64, 2 * rr, :], in_=vs_m2[r:1024:16, :]),
                                  reads=rd, writes=[t_vbuf[vb]])
                            P.dma("sp", lambda e, vb=vb, r=r, rr=rr, vs_m1=vs_m1: e.dma_start(out=vbuf[vb][64:128, 2 * rr, :], in_=vs_m1[r:1024:16, :]),
                                  reads=rd, writes=[t_vbuf[vb]])
                            P.dma("sp", lambda e, vb=vb, r=r, rr=rr, vs_own=vs_own: e.dma_start(out=vbuf[vb][0:64, 2 * rr + 1, :], in_=vs_own[r:1024:16, :]),
                                  reads=rd, writes=[t_vbuf[vb]])
                        for hl in range(4):
                            for rr in range(4):
                                r = r0 + rr
                                block(hl, hg * 4 + hl, 16, 64,
                                      [(slice(r, r + 2033, 16), 128), (slice(2048 + r, 2048 + r + 1009, 16), 64)],
                                      [2 * rr, 2 * rr + 1], slice(r, r + 1009, 16), [C_KB + 2, None], vb, False)
            flush()
            for hl in range(4):
                hh = hg * 4 + hl
                P.op("dve", lambda e, hl=hl: e.reciprocal(out=rden[:], in_=numden[:, hl, 1, :]), reads=[t_nd[hl]], writes=[t_rden])
                P.op("dve", lambda e, hl=hl, hh=hh: e.tensor_tensor(out=attnT[:, hh, :], in0=numden[:, hl, 0, :], in1=rden[:], op=ALU.mult),
                     reads=[t_nd[hl], t_rden], writes=[t_attnT])
        if self.stop == "attn_only":
            ao = nc.dram_tensor("attnT_out", [128, 16, T], F32, kind="ExternalOutput").ap()
            t_ao = Tile("ao")
            for hh in range(16):
                P.op("act", lambda e, hh=hh: e.copy(out=rden[:], in_=attnT[:, hh, :]), reads=[t_attnT], writes=[t_rden])
                P.dma("sp", lambda e, hh=hh: e.dma_start(out=ao[:, hh, :], in_=rden[:]), reads=[t_rden], writes=[t_ao])
            P.wait_all("sp", [t_ao])
        P.barrier()
        for i in range(NT):
            P.dma("sp", lambda e, i=i: e.dma_start(out=self.h[:, i, :], in_=hspill[i * 128:(i + 1) * 128, :]),
                  reads=[t_spill], writes=[self.t_h[i]])
        for np_ in range(4):
            s_ = self.next_slot()
            s3 = self.slot3(s_, KC, 512)
            self.load_w(wor[:, :, np_ * 512:(np_ + 1) * 512], s_, s3)
            for i in range(NT):
                b = self.bank()
                for hh in range(16):
                    P.op("pe", lambda e, i=i, hh=hh, b=b, s3=s3: e.matmul(
                        self.ps[:, b, :], lhsT=attnT[:, hh, i * 128:(i + 1) * 128], rhs=s3[:, hh, :],
                        start=(hh == 0), stop=(hh == 15)),
                        reads=[t_attnT, self.t_slots[s_]], writes=[self.t_ps[b]], signal=(hh == 15))
                P.op("dve", lambda e, i=i, np_=np_, b=b: e.tensor_tensor(
                    out=self.h[:, i, np_ * 512:(np_ + 1) * 512], in0=self.ps[:, b, :],
                    in1=self.h[:, i, np_ * 512:(np_ + 1) * 512], op=ALU.add),
                    reads=[self.t_ps[b], self.t_h[i]], writes=[self.t_h[i]])
        P.barrier()
        es.close()

    def final(self):
        P = self.P
        nc = self.nc
        gfin_d = self.inp("gfin", [128, D])
        y = nc.dram_tensor("y", [T, D], F32, kind="ExternalOutput").ap()
        es = ExitStack()
        self.es.enter_context(es)
        gfin = es.enter_context(nc.sbuf_tensor("fin_g", [128, D], F32)); t_g = Tile("gfin")
        ob = [es.enter_context(nc.sbuf_tensor("fin_o%d" % i, [128, D], F32)) for i in range(2)]
        t_ob = [Tile("ob0"), Tile("ob1")]
        t_y = Tile("y")
        P.dma("sp", lambda e: e.dma_start(out=gfin[:], in_=gfin_d[:, :]), writes=[t_g])
        for i in range(NT):
            o = ob[i % 2]; to = t_ob[i % 2]
            P.op("act", lambda e, i=i, o=o: e.activation(out=o[:], in_=self.h[:, i, :], func=AF.Square, accum_out=self.stat[:, 0:1]),
                 reads=[self.t_h[i]], writes=[to, self.t_stat])
            P.op("dve", lambda e: e.tensor_scalar(out=self.stat[:, 1:2], in0=self.stat[:, 0:1], scalar1=1.0 / D, scalar2=EPS,
                                                  op0=ALU.mult, op1=ALU.add), reads=[self.t_stat], writes=[self.t_stat])
            P.op("act", lambda e: e.activation(out=self.stat[:, 2:3], in_=self.stat[:, 1:2], func=AF.Sqrt),
                 reads=[self.t_stat], writes=[self.t_stat])
            P.op("dve", lambda e: e.reciprocal(out=self.stat[:, 3:4], in_=self.stat[:, 2:3]), reads=[self.t_stat], writes=[self.t_stat])
            P.op("dve", lambda e, i=i, o=o: e.scalar_tensor_tensor(out=o[:], in0=self.h[:, i, :], scalar=self.stat[:, 3:4], in1=gfin[:],
                                                                  op0=ALU.mult, op1=ALU.mult),
                 reads=[self.t_h[i], self.t_stat, t_g], writes=[to])
            P.dma("sp", lambda e, i=i, o=o: e.dma_start(out=y[i * 128:(i + 1) * 128, :], in_=o[:]), reads=[to], writes=[t_y])
        P.wait_all("sp", [t_y])
        es.close()

    def store_h(self):
        P = self.P
        y = self.nc.dram_tensor("y", [T, D], F32, kind="ExternalOutput").ap()
        t_y = Tile("y")
        for i in range(NT):
            P.dma("sp", lambda e, i=i: e.dma_start(out=y[i * 128:(i + 1) * 128, :], in_=self.h[:, i, :]),
                  reads=[self.t_h[i]], writes=[t_y])
        P.wait_all("sp", [t_y])

    def build(self):
        import os
        DBG = int(os.environ.get("KDBG", "99"))
        self.load_inputs()
        self.dyn_init()
        if self.stop == "attn_only":
            xin = self.inp("xn_in", [128, KC, T + 2], BF16)
            self.P.dma("sp", lambda e: e.dma_start(out=self.xnT[:], in_=xin[:, :, :]), writes=[self.t_xnT])
            self.kv()
            self.attn()
            self.store_h()
            self.P.finalize(); self.es.close(); return self.nc
        if DBG >= 1:
            self.norm(C_GAIN + 0)
        if DBG >= 2:
            self.gla()
        if self.stop == "gla":
            self.store_h()
            self.P.finalize(); self.es.close(); return self.nc
        self.norm(C_GAIN + 16)
        self.ffn(0)
        if self.stop == "ffn0":
            self.store_h()
            self.P.finalize(); self.es.close(); return self.nc
        self.norm(C_GAIN + 32)
        self.kv()
        self.norm(C_GAIN + 48)
        self.attn()
        if self.stop == "attn":
            self.store_h()
            self.P.finalize(); self.es.close(); return self.nc
        self.norm(C_GAIN + 64)
        self.ffn(1)
        if self.stop == "ffn1":
            self.store_h()
            self.P.finalize(); self.es.close(); return self.nc
        self.final()
        self.P.finalize()
        self.es.close()
        return self.nc


def _pc(v):
    return np.ascontiguousarray(v.reshape(-1, 128).T)


def make_consts(inputs, c):
    j = c % 4
    cst = np.zeros((128, NCST), np.float32)
    gains = [inputs["attn_norm"][0], inputs["ffn_norm"][0], inputs["kv_norm"], inputs["attn_norm"][1], inputs["ffn_norm"][1]]
    for g, v in enumerate(gains):
        cst[:, C_GAIN + 16 * g:C_GAIN + 16 * (g + 1)] = _pc(np.asarray(v))
    cst[:, C_BA2:C_BA2 + 8] = _pc(np.asarray(inputs["gla_b_a2"][0]))
    for l in range(2):
        for tap in range(3):
            o = C_CONVW + (l * 3 + tap) * 44
            cst[:, o:o + 44] = _pc(np.asarray(inputs["ffn_conv_w"][l, tap]))
        o = C_CONVB + l * 44
        cst[:, o:o + 44] = _pc(np.asarray(inputs["ffn_conv_b"][l]))
    for ri in range(3):
        cst[:, C_SEL + ri] = 1.0 if ri < j else 0.0
    cst[:, C_HALO] = 1.0 if j > 0 else 0.0
    NEG = -30000.0
    cst[:, C_KB + 0] = NEG if j == 0 else 0.0
    cst[:, C_KB + 1] = NEG if j == 0 else 0.0
    kb16 = np.zeros(128, np.float32)
    if j == 0:
        kb16[:] = NEG
    elif j == 1:
        kb16[:64] = NEG
    cst[:, C_KB + 2] = kb16
    return cst


def kernel(_stop="full", **inputs):
    inputs = {k: np.asarray(v) for k, v in inputs.items()}
    x = inputs["x"]
    kb = K(stop=_stop)
    nc = kb.build()
    ident = np.eye(128, dtype=np.float32)
    cmask = np.triu(np.ones((128, 128), np.float32))
    hnb = np.ascontiguousarray(np.broadcast_to(inputs["gla_head_norm"][0][None, :], (128, 512))).astype(np.float32)
    BIG = 1.0e6
    kk = np.arange(128)[:, None]; qq = np.arange(128)[None, :]
    jprev = np.where(kk >= qq, (qq - kk + 128).astype(np.float32), BIG)
    jcur = np.where(kk <= qq, (qq - kk).astype(np.float32), BIG)
    jtile = np.ascontiguousarray(np.concatenate([jprev, jcur], axis=1)).astype(np.float32)
    gfin = np.ascontiguousarray(np.broadcast_to(inputs["final_norm"][None, :], (128, D))).astype(np.float32)
    shared = {
        "ident": ident, "cmask": cmask, "hnb": hnb,
        "gla_w_in": inputs["gla_w_in"][0], "gla_w_a2": inputs["gla_w_a2"][0], "gla_w_out": inputs["gla_w_out"][0],
        "w_kv": inputs["w_kv"], "dsa_w_q": inputs["dsa_w_q"][0], "dsa_w_out": inputs["dsa_w_out"][0],
        "jtile": jtile, "gfin": gfin,
        "ffn_w_up0": inputs["ffn_w_up"][0], "ffn_w_down0": inputs["ffn_w_down"][0],
        "ffn_w_up1": inputs["ffn_w_up"][1], "ffn_w_down1": inputs["ffn_w_down"][1],
    }
    in_maps = []
    for c in range(8):
        b, j = c // 4, c % 4
        m = {k: v for k, v in shared.items() if k in kb.din}
        m["x"] = np.ascontiguousarray(x[b, j * T:(j + 1) * T, :])
        m["cst"] = make_consts(inputs, c)
        gb = 4 * b
        m["idx"] = np.array([[max(c - 1, 0), max(c - 2, 0), gb, 0, 0, 0, 0, 0]], dtype=np.int32)
        if "xn_in" in kb.din:
            m["xn_in"] = inputs["_xn_in"][c]
        in_maps.append(m)
    res = run_bass_kernel_spmd(nc, in_maps, core_ids=list(range(8)))
    if _stop == "attn_only":
        return [res.results[c]["attnT_out"] for c in range(8)]
    out = np.zeros(x.shape, np.float32)
    for c in range(8):
        b, j = c // 4, c % 4
        out[b, j * T:(j + 1) * T, :] = res.results[c]["y"]
    return out
```
